# Optimizing a Trainium2 kernel written in Bass

```python
import math
import jax, jax.numpy as jnp
from jax import lax
import numpy as np

D_MODEL = 1024
BATCH = 1
SEQ = 16384
DEPTH = 4

N_A_LAYERS = DEPTH // 2
N_B_LAYERS = DEPTH - N_A_LAYERS
D_FF = 2816
GM_WIDTH = D_MODEL
GM_GROUPS = 8
GM_GROUP_DIM = GM_WIDTH // GM_GROUPS
GM_CHUNK = 128
N_HEADS = 16
HEAD_DIM = D_MODEL // N_HEADS
ATTN_WIDTH = N_HEADS * HEAD_DIM
MOBA_BLOCK = 256
MOBA_TOPK = 3
MOBA_QUERY_CHUNK = 64
REL_BUCKETS = 32
REL_MAX_DIST = 128
DEEPNORM_ALPHA = (2 * DEPTH) ** 0.25
DEEPNORM_BETA = (8 * DEPTH) ** -0.25
LN_EPS = 1e-5

kernel_name = "yoco_gmlp_moba_macaron_deepnorm"


def layer_norm(x, g, b):
    xf = x.astype(jnp.float32)
    mu = jnp.mean(xf, axis=-1, keepdims=True)
    var = jnp.mean(jnp.square(xf - mu), axis=-1, keepdims=True)
    y = (xf - mu) * lax.rsqrt(var + LN_EPS)
    return (y * g.astype(jnp.float32) + b.astype(jnp.float32)).astype(x.dtype)


def swiglu_ffn(x, w_gate, w_up, w_down):
    return (jax.nn.silu(x @ w_gate) * (x @ w_up)) @ w_down


def t5_bucket(dist):
    n = jnp.maximum(dist, 0)
    max_exact = REL_BUCKETS // 2
    nf = jnp.maximum(n, 1).astype(jnp.float32)
    large = max_exact + (jnp.log(nf / max_exact) / math.log(REL_MAX_DIST / max_exact)
                         * (REL_BUCKETS - max_exact)).astype(jnp.int32)
    large = jnp.minimum(large, REL_BUCKETS - 1)
    return jnp.where(n < max_exact, n, large)


def gmlp_mixer(x, w_in, sgu_g, sgu_b, w_s, b_s, w_out):
    B, S, _ = x.shape
    z = jax.nn.gelu(x @ w_in, approximate=False)
    u, v = jnp.split(z, 2, axis=-1)
    v = layer_norm(v, sgu_g, sgu_b)
    v = v.reshape(B, S // GM_CHUNK, GM_CHUNK, GM_GROUPS, GM_GROUP_DIM)
    causal = jnp.tril(jnp.ones((GM_CHUNK, GM_CHUNK), dtype=bool))
    w = jnp.where(causal[None], w_s, 0)
    mixed = jnp.einsum('gts,bcsgd->bctgd', w, v) + b_s.T[None, None, :, :, None]
    y = u * mixed.reshape(B, S, GM_WIDTH)
    return y @ w_out


def moba_core(q, k_blocks, v_blocks, k_mean, rel_bias):
    S_pad = q.shape[0]
    nblk = S_pad // MOBA_BLOCK
    topk = min(MOBA_TOPK, nblk)
    n_chunks = S_pad // MOBA_QUERY_CHUNK
    scale = HEAD_DIM ** -0.5
    hidx = jnp.arange(N_HEADS)[None, :, None]
    blk_ar = jnp.arange(MOBA_BLOCK)

    def chunk_fn(ci):
        start = ci * MOBA_QUERY_CHUNK
        qc = lax.dynamic_slice_in_dim(q, start, MOBA_QUERY_CHUNK, 0)
        qpos = start + jnp.arange(MOBA_QUERY_CHUNK)
        qblk = start // MOBA_BLOCK
        gate = jnp.einsum('chd,hnd->chn', qc, k_mean).astype(jnp.float32)
        past = jnp.arange(nblk) < qblk
        gate = jnp.where(past[None, None, :], gate, -jnp.inf)
        _, idx = lax.top_k(gate, topk)
        valid = idx < qblk
        kg = k_blocks[hidx, idx]
        vg = v_blocks[hidx, idx]
        s_sel = jnp.einsum('chd,chjkd->chjk', qc, kg).astype(jnp.float32) * scale
        kpos_sel = idx[..., None] * MOBA_BLOCK + blk_ar
        bias_sel = rel_bias[hidx[..., None], t5_bucket(qpos[:, None, None, None] - kpos_sel)]
        s_sel = jnp.where(valid[..., None], s_sel + bias_sel.astype(jnp.float32), -jnp.inf)
        k_own = lax.dynamic_index_in_dim(k_blocks, qblk, axis=1, keepdims=False)
        v_own = lax.dynamic_index_in_dim(v_blocks, qblk, axis=1, keepdims=False)
        kpos_own = qblk * MOBA_BLOCK + blk_ar
        d_own = qpos[:, None] - kpos_own[None, :]
        bias_own = jnp.transpose(rel_bias[:, t5_bucket(d_own)], (1, 0, 2))
        s_own = (jnp.einsum('chd,hkd->chk', qc, k_own).astype(jnp.float32) * scale
                 + bias_own.astype(jnp.float32))
        s_own = jnp.where((d_own >= 0)[:, None, :], s_own, -jnp.inf)
        logits = jnp.concatenate(
            [s_sel.reshape(MOBA_QUERY_CHUNK, N_HEADS, topk * MOBA_BLOCK), s_own], axis=-1)
        p = jax.nn.softmax(logits, axis=-1).astype(q.dtype)
        p_sel = p[..., :topk * MOBA_BLOCK]
        p_own = p[..., topk * MOBA_BLOCK:]
        out = (jnp.einsum('chn,chnd->chd', p_sel,
                          vg.reshape(MOBA_QUERY_CHUNK, N_HEADS, topk * MOBA_BLOCK, HEAD_DIM))
               + jnp.einsum('chk,hkd->chd', p_own, v_own))
        return out

    out = lax.map(chunk_fn, jnp.arange(n_chunks))
    return out.reshape(S_pad, N_HEADS, HEAD_DIM)


def moba_mixer(x, w_q, w_o, k_blocks, v_blocks, k_mean, rel_bias):
    B, S, _ = x.shape
    S_pad = k_blocks.shape[2] * MOBA_BLOCK
    q = (x @ w_q).reshape(B, S, N_HEADS, HEAD_DIM)
    q = jnp.pad(q, ((0, 0), (0, S_pad - S), (0, 0), (0, 0)))
    o = jax.vmap(moba_core, in_axes=(0, 0, 0, 0, None))(q, k_blocks, v_blocks, k_mean, rel_bias)
    return o[:, :S].reshape(B, S, ATTN_WIDTH) @ w_o


def shared_kv(x, w_k, w_v):
    B, S, _ = x.shape
    S_pad = -(-S // MOBA_BLOCK) * MOBA_BLOCK
    nblk = S_pad // MOBA_BLOCK
    pad = ((0, 0), (0, S_pad - S), (0, 0), (0, 0))
    k = jnp.pad((x @ w_k).reshape(B, S, N_HEADS, HEAD_DIM), pad)
    v = jnp.pad((x @ w_v).reshape(B, S, N_HEADS, HEAD_DIM), pad)
    k_blocks = jnp.transpose(k.reshape(B, nblk, MOBA_BLOCK, N_HEADS, HEAD_DIM), (0, 3, 1, 2, 4))
    v_blocks = jnp.transpose(v.reshape(B, nblk, MOBA_BLOCK, N_HEADS, HEAD_DIM), (0, 3, 1, 2, 4))
    k_mean = jnp.mean(k_blocks.astype(jnp.float32), axis=3).astype(k.dtype)
    return k_blocks, v_blocks, k_mean


def setup_inputs(seed: int = 0) -> dict:
    key = jax.random.key(seed)
    ks = jax.random.split(key, 20)
    f32 = jnp.float32
    nrm = lambda k, shape, s: jax.random.normal(k, shape, f32) * s
    return {
        "x": nrm(ks[0], (BATCH, SEQ, D_MODEL), 1.0),
        "ln_g": 1.0 + nrm(ks[1], (DEPTH, 3, D_MODEL), 0.02),
        "ln_b": nrm(ks[2], (DEPTH, 3, D_MODEL), 0.02),
        "ffn_w_gate": nrm(ks[3], (DEPTH, 2, D_MODEL, D_FF), D_MODEL ** -0.5),
        "ffn_w_up": nrm(ks[4], (DEPTH, 2, D_MODEL, D_FF), D_MODEL ** -0.5),
        "ffn_w_down": nrm(ks[5], (DEPTH, 2, D_FF, D_MODEL), D_FF ** -0.5 * DEEPNORM_BETA),
        "gm_w_in": nrm(ks[6], (N_A_LAYERS, D_MODEL, 2 * GM_WIDTH), D_MODEL ** -0.5),
        "gm_sgu_g": 1.0 + nrm(ks[7], (N_A_LAYERS, GM_WIDTH), 0.02),
        "gm_sgu_b": nrm(ks[8], (N_A_LAYERS, GM_WIDTH), 0.02),
        "gm_w_s": nrm(ks[9], (N_A_LAYERS, GM_GROUPS, GM_CHUNK, GM_CHUNK), GM_CHUNK ** -0.5),
        "gm_b_s": 1.0 + nrm(ks[10], (N_A_LAYERS, GM_GROUPS, GM_CHUNK), 0.02),
        "gm_w_out": nrm(ks[11], (N_A_LAYERS, GM_WIDTH, D_MODEL), GM_WIDTH ** -0.5 * DEEPNORM_BETA),
        "attn_w_q": nrm(ks[12], (N_B_LAYERS, D_MODEL, ATTN_WIDTH), D_MODEL ** -0.5),
        "attn_w_o": nrm(ks[13], (N_B_LAYERS, ATTN_WIDTH, D_MODEL), ATTN_WIDTH ** -0.5 * DEEPNORM_BETA),
        "w_k_shared": nrm(ks[14], (D_MODEL, ATTN_WIDTH), D_MODEL ** -0.5),
        "w_v_shared": nrm(ks[15], (D_MODEL, ATTN_WIDTH), D_MODEL ** -0.5 * DEEPNORM_BETA),
        "rel_bias": nrm(ks[16], (N_HEADS, REL_BUCKETS), 0.1),
    }


def reference(x, ln_g, ln_b, ffn_w_gate, ffn_w_up, ffn_w_down,
              gm_w_in, gm_sgu_g, gm_sgu_b, gm_w_s, gm_b_s, gm_w_out,
              attn_w_q, attn_w_o, w_k_shared, w_v_shared, rel_bias):
    k_blocks = v_blocks = k_mean = None
    for layer in range(DEPTH):
        h = swiglu_ffn(x, ffn_w_gate[layer, 0], ffn_w_up[layer, 0], ffn_w_down[layer, 0])
        x = layer_norm(DEEPNORM_ALPHA * x + 0.5 * h, ln_g[layer, 0], ln_b[layer, 0])
        if layer < N_A_LAYERS:
            a = layer
            mix = gmlp_mixer(x, gm_w_in[a], gm_sgu_g[a], gm_sgu_b[a], gm_w_s[a], gm_b_s[a], gm_w_out[a])
        else:
            j = layer - N_A_LAYERS
            mix = moba_mixer(x, attn_w_q[j], attn_w_o[j], k_blocks, v_blocks, k_mean, rel_bias)
        x = layer_norm(DEEPNORM_ALPHA * x + mix, ln_g[layer, 1], ln_b[layer, 1])
        h = swiglu_ffn(x, ffn_w_gate[layer, 1], ffn_w_up[layer, 1], ffn_w_down[layer, 1])
        x = layer_norm(DEEPNORM_ALPHA * x + 0.5 * h, ln_g[layer, 2], ln_b[layer, 2])
        if layer == N_A_LAYERS - 1:
            k_blocks, v_blocks, k_mean = shared_kv(x, w_k_shared, w_v_shared)
    return x
```

```python
import os
from concourse.bass_utils import run_bass_kernel_spmd
import concourse.bass as bass
import concourse.mybir as mybir

ENGS = ("pe", "act", "dve", "pool", "sp")
EPOCH = 16000


class Res:
    __slots__ = ("name", "w", "r", "excl")

    def __init__(self, name, excl=False):
        self.name = name
        self.w = None
        self.r = []
        self.excl = excl


class Op:
    __slots__ = ("eng", "fn", "dma_key", "deps", "idx", "sig", "dma_val", "tag")

    def __init__(self, eng, fn, dma_key, tag):
        self.eng = eng
        self.fn = fn
        self.dma_key = dma_key
        self.deps = []
        self.idx = -1
        self.sig = 0
        self.dma_val = 0
        self.tag = tag


class Prog:
    def __init__(self, nc, sync_same_engine=True):
        self.nc = nc
        self.ops = {e: [] for e in ENGS}
        self.dma_cnt = {}
        self.sync_same = sync_same_engine

    def op(self, eng, fn, reads=(), writes=(), dma_key=None, tag=""):
        o = Op(eng, fn, dma_key, tag)
        deps = []
        for r in reads:
            if r.w is not None:
                deps.append((r.w, True))
            if r.excl:
                deps.extend((x, True) for x in r.r if x.eng != eng)
        for w in writes:
            if w.w is not None:
                deps.append((w.w, False))
            deps.extend((x, False) for x in w.r)
        seen = set()
        for d, raw in deps:
            if id(d) in seen:
                continue
            if d.dma_key is None and d.eng == eng and dma_key is None:
                if eng == "pe" or not raw:
                    continue
            seen.add(id(d))
            o.deps.append(d)
        for r in reads:
            r.r.append(o)
        for w in writes:
            w.w = o
            w.r = []
        if dma_key is not None:
            self.dma_cnt[dma_key] = self.dma_cnt.get(dma_key, 0) + 16
            o.dma_val = self.dma_cnt[dma_key]
        o.idx = len(self.ops[eng])
        self.ops[eng].append(o)
        return o

    def emit(self, final_waits=()):
        nc = self.nc
        need = set()
        for e in ENGS:
            for o in self.ops[e]:
                for d in o.deps:
                    if d.dma_key is None:
                        need.add(id(d))
        for e, o in final_waits:
            if o.dma_key is None:
                need.add(id(o))
        nep = {}
        for e in ENGS:
            c = 0
            for o in self.ops[e]:
                if o.dma_key is None and id(o) in need:
                    c += 1
                    o.sig = c
            nep[e] = c // EPOCH + 1
        from contextlib import ExitStack
        with ExitStack() as st:
            esem = {e: [st.enter_context(nc.semaphore("s_%s%d" % (e, i))) for i in range(nep[e])] for e in ENGS}
            dsem = {k: st.enter_context(nc.semaphore("d_%s" % (str(k).replace(" ", "").replace("'", "").replace("(", "_").replace(")", "_").replace(",", "_"))))
                    for k in self.dma_cnt}
            block = st.enter_context(nc.Block())

            def run(ename, engobj):
                waited = {}
                for o in self.ops[ename]:
                    for d in o.deps:
                        if d.dma_key is not None:
                            s, v = dsem[d.dma_key], d.dma_val
                        else:
                            s, v = esem[d.eng][(d.sig - 1) // EPOCH], (d.sig - 1) % EPOCH + 1
                        if waited.get(id(s), 0) >= v:
                            continue
                        waited[id(s)] = v
                        engobj.wait_ge(s, v)
                    ins = o.fn(engobj)
                    if o.dma_key is not None:
                        ins.then_inc(dsem[o.dma_key], 16)
                    elif o.sig:
                        ins.then_inc(esem[ename][(o.sig - 1) // EPOCH], 1)
                for e2, o in final_waits:
                    if e2 != ename:
                        continue
                    if o.dma_key is not None:
                        engobj.wait_ge(dsem[o.dma_key], o.dma_val)
                    else:
                        engobj.wait_ge(esem[o.eng][(o.sig - 1) // EPOCH], (o.sig - 1) % EPOCH + 1)

            @block.tensor
            def _(pe):
                run("pe", pe)

            @block.scalar
            def _(act):
                run("act", act)

            @block.vector
            def _(dve):
                run("dve", dve)

            @block.gpsimd
            def _(pool):
                run("pool", pool)

            @block.sync
            def _(sp):
                run("sp", sp)


import numpy as np
import concourse.bass as bass
import concourse.mybir as mybir
from contextlib import ExitStack

F32 = mybir.dt.float32
BF16 = mybir.dt.bfloat16
AF = mybir.ActivationFunctionType
ALU = mybir.AluOpType

D = 1024
DFF = 2816
NFC = DFF // 128
TOK = 2048
ALPHA = float((2 * 4) ** 0.25)
EPS = 1e-5


class KB:
    def __init__(self):
        self.nc = bass.Bass("TRN2", target_bir_lowering=False)
        self.P = Prog(self.nc)
        self.st = ExitStack()
        self.res = {}

    def R(self, name):
        if name not in self.res:
            self.res[name] = Res(name, excl=name.startswith(("pG", "pU", "pD", "pT", "pM", "bank")))
        return self.res[name]

    def sb(self, name, shape, dt):
        return self.st.enter_context(self.nc.sbuf_tensor("sb_" + name, shape, dt))

    def ps(self, name, shape, dt=F32):
        return self.st.enter_context(self.nc.psum_tensor("ps_" + name, shape, dt))

    def din(self, name, shape, dt=F32):
        return self.nc.dram_tensor(name, list(shape), dt, kind="ExternalInput").ap()

    def dout(self, name, shape, dt=F32):
        return self.nc.dram_tensor(name, list(shape), dt, kind="ExternalOutput").ap()


def emit_ln_tile(kb, xt_ap, xres, gam, bet, scr, slot, out_ap, out_res, tagp, gname="gam", bname="bet"):
    P = kb.P
    R = kb.R
    stats, mv, rstd, nb, xh = scr["stats"][slot], scr["mv"][slot], scr["rstd"][slot], scr["nb"][slot], scr["xh"][slot]
    rs, rmv, rr, rn, rxh = (R("%s_stats%d" % (tagp, slot)), R("%s_mv%d" % (tagp, slot)), R("%s_rstd%d" % (tagp, slot)),
                            R("%s_nb%d" % (tagp, slot)), R("%s_xh%d" % (tagp, slot)))
    P.op("dve", lambda e: e.bn_stats(stats[:, 0:6], xt_ap[:, 0:512]), reads=[xres], writes=[rs])
    P.op("dve", lambda e: e.bn_stats(stats[:, 6:12], xt_ap[:, 512:1024]), reads=[xres], writes=[rs])
    P.op("dve", lambda e: e.bn_aggr(mv[:], stats[:]), reads=[rs], writes=[rmv])
    P.op("act", lambda e: e.activation(rstd[:], mv[:, 1:2], AF.Sqrt, bias=scr["eps"][:, 0:1], scale=1.0), reads=[rmv, R("epsc")], writes=[rr])
    P.op("dve", lambda e: e.reciprocal(rstd[:], rstd[:]), reads=[rr], writes=[rr])
    P.op("dve", lambda e: e.tensor_scalar(nb[:], mv[:, 0:1], -1.0, rstd[:, 0:1], ALU.mult, ALU.mult), reads=[rmv, rr], writes=[rn])
    P.op("act", lambda e: e.activation(xh[:], xt_ap, AF.Identity, bias=nb[:, 0:1], scale=rstd[:, 0:1]), reads=[xres, rr, rn], writes=[rxh])
    P.op("dve", lambda e: e.tensor_tensor(xh[:], xh[:], gam[:], ALU.mult), reads=[rxh, R(gname)], writes=[rxh])
    P.op("pool", lambda e: e.tensor_tensor(out_ap, xh[:], bet[:], ALU.add), reads=[rxh, R(bname)], writes=[out_res])


def alloc_ln_scratch(kb, tagp, nslot=2):
    scr = {k: [] for k in ("stats", "mv", "rstd", "nb", "xh")}
    scr["eps"] = kb.sb("%s_eps" % tagp, [128, 1], F32)
    kb.P.op("dve", lambda e: e.memset(scr["eps"][:], EPS), writes=[kb.R("epsc")])
    for s in range(nslot):
        scr["stats"].append(kb.sb("%s_stats%d" % (tagp, s), [128, 12], F32))
        scr["mv"].append(kb.sb("%s_mv%d" % (tagp, s), [128, 2], F32))
        scr["rstd"].append(kb.sb("%s_rstd%d" % (tagp, s), [128, 1], F32))
        scr["nb"].append(kb.sb("%s_nb%d" % (tagp, s), [128, 1], F32))
        scr["xh"].append(kb.sb("%s_xh%d" % (tagp, s), [128, 1024], F32))
    return scr


def emit_load_transpose(kb, xg, xgres, xT, xTres, ident, pT, ntile, evac_cnt=[0]):
    P = kb.P
    R = kb.R
    for t in range(ntile):
        for h4 in range(2):
            s = evac_cnt[0] % 2
            evac_cnt[0] += 1
            pt = pT[s]
            rpt = R("pT%d" % s)
            for q in range(4):
                kc = h4 * 4 + q
                P.op("pe", lambda e, t=t, kc=kc, q=q, pt=pt: e.transpose(pt[:, q, :], xg[:, t, kc * 128:(kc + 1) * 128], ident[:]),
                     reads=[xgres, R("ident")], writes=[rpt])
            if s == 0:
                P.op("dve", lambda e, t=t, h4=h4, pt=pt: e.tensor_copy(xT[:, h4 * 4:(h4 + 1) * 4, t * 128:(t + 1) * 128], pt[:]),
                     reads=[rpt], writes=[xTres])
            else:
                P.op("act", lambda e, t=t, h4=h4, pt=pt: e.copy(xT[:, h4 * 4:(h4 + 1) * 4, t * 128:(t + 1) * 128], pt[:]),
                     reads=[rpt], writes=[xTres])


def build_ffn():
    kb = KB()
    nc, P, R = kb.nc, kb.P, kb.R
    x = kb.din("x", [TOK, D])
    wg = kb.din("wg", [D, DFF])
    wu = kb.din("wu", [D, DFF])
    wd = kb.din("wd", [DFF, D])
    lng = kb.din("lng", [1, D])
    lnb = kb.din("lnb", [1, D])
    identd = kb.din("ident", [128, 128])
    y = kb.dout("y", [TOK, D])
    GT = 4
    NG = TOK // (128 * GT)
    with kb.st:
        ident = kb.sb("ident", [128, 128], F32)
        gam = kb.sb("gam", [128, D], F32)
        bet = kb.sb("bet", [128, D], F32)
        wd_sb = kb.sb("wd_sb", [128, NFC, D], BF16)
        xg = kb.sb("xg", [128, GT, D], F32)
        xT = kb.sb("xT", [128, 8, GT * 128], BF16)
        h1T = kb.sb("h1T", [128, NFC, GT * 128], BF16)
        wgs = [kb.sb("wgs%d" % i, [128, 8, 256], BF16) for i in range(2)]
        wus = [kb.sb("wus%d" % i, [128, 8, 256], BF16) for i in range(2)]
        tmp = [kb.sb("tmp%d" % i, [128, 512], F32) for i in range(2)]
        yo = [kb.sb("yo%d" % i, [128, D], F32) for i in range(2)]
        scr = alloc_ln_scratch(kb, "ln")
        pT = [kb.ps("pT%d" % i, [128, 4, 128]) for i in range(2)]
        pG = [kb.ps("pG%d" % i, [128, 512]) for i in range(2)]
        pU = [kb.ps("pU%d" % i, [128, 512]) for i in range(2)]
        pD = [kb.ps("pD%d" % i, [128, 512]) for i in range(2)]

        P.op("sp", lambda e: e.dma_start(out=ident[:], in_=identd[:, :]), writes=[R("ident")], dma_key="ident")
        P.op("sp", lambda e: e.dma_start(out=gam[:], in_=lng.partition_broadcast(128)), writes=[R("gam")], dma_key="gam")
        P.op("sp", lambda e: e.dma_start(out=bet[:], in_=lnb.partition_broadcast(128)), writes=[R("bet")], dma_key="bet")
        wdv = wd.rearrange("(c p) n -> p c n", p=128)
        for i in range(2):
            P.op("pool", lambda e, i=i: e.dma_start(out=wd_sb[:, i * 11:(i + 1) * 11, :], in_=wdv[:, i * 11:(i + 1) * 11, :]),
                 writes=[R("wd_sb")], dma_key="wd%d" % i)
        wgv = wg.rearrange("(c p) f -> p c f", p=128)
        wuv = wu.rearrange("(c p) f -> p c f", p=128)
        outs = []
        gu = 0
        dcnt = 0
        for g in range(NG):
            xv = x[g * GT * 128:(g + 1) * GT * 128, :].rearrange("(t p) d -> p t d", p=128)
            P.op("sp", lambda e, xv=xv: e.dma_start(out=xg[:], in_=xv), writes=[R("xg")], dma_key="xg")
            emit_load_transpose(kb, xg, R("xg"), xT, R("xT"), ident, pT, GT)
            for fp in range(NFC // 2):
                s = fp % 2
                P.op("pool", lambda e, s=s, fp=fp: e.dma_start(out=wgs[s][:], in_=wgv[:, :, fp * 256:(fp + 1) * 256]),
                     writes=[R("wgs%d" % s)], dma_key="wgs%d" % s)
                P.op("pool", lambda e, s=s, fp=fp: e.dma_start(out=wus[s][:], in_=wuv[:, :, fp * 256:(fp + 1) * 256]),
                     writes=[R("wus%d" % s)], dma_key="wus%d" % s)
                for fi in range(2):
                    fc = 2 * fp + fi
                    s2 = gu % 2
                    gu += 1
                    for kc in range(8):
                        P.op("pe", lambda e, s=s, s2=s2, kc=kc, fi=fi: e.matmul(pG[s2][:], wgs[s][:, kc, fi * 128:(fi + 1) * 128], xT[:, kc, :],
                                                                                  start=(kc == 0), stop=(kc == 7)),
                             reads=[R("wgs%d" % s), R("xT")], writes=[R("pG%d" % s2)])
                    for kc in range(8):
                        P.op("pe", lambda e, s=s, s2=s2, kc=kc, fi=fi: e.matmul(pU[s2][:], wus[s][:, kc, fi * 128:(fi + 1) * 128], xT[:, kc, :],
                                                                                  start=(kc == 0), stop=(kc == 7)),
                             reads=[R("wus%d" % s), R("xT")], writes=[R("pU%d" % s2)])
                    P.op("act", lambda e, s2=s2: e.activation(tmp[s2][:], pG[s2][:], AF.Silu), reads=[R("pG%d" % s2)], writes=[R("tmp%d" % s2)])
                    P.op("dve", lambda e, s2=s2, fc=fc: e.scalar_tensor_tensor(h1T[:, fc, :], tmp[s2][:], 0.5, pU[s2][:], ALU.mult, ALU.mult),
                         reads=[R("tmp%d" % s2), R("pU%d" % s2)], writes=[R("h1T")])
            for t in range(GT):
                for half in range(2):
                    sd = dcnt % 2
                    dcnt += 1
                    for fc in range(NFC):
                        P.op("pe", lambda e, sd=sd, fc=fc, t=t, half=half: e.matmul(pD[sd][:], h1T[:, fc, t * 128:(t + 1) * 128],
                                                                                      wd_sb[:, fc, half * 512:(half + 1) * 512],
                                                                                      start=(fc == 0), stop=(fc == NFC - 1)),
                             reads=[R("h1T"), R("wd_sb")], writes=[R("pD%d" % sd)])
                    P.op("dve", lambda e, sd=sd, t=t, half=half: e.scalar_tensor_tensor(xg[:, t, half * 512:(half + 1) * 512],
                                                                                          xg[:, t, half * 512:(half + 1) * 512], ALPHA, pD[sd][:],
                                                                                          ALU.mult, ALU.add),
                         reads=[R("pD%d" % sd), R("xg")], writes=[R("xg")])
                so = (g * GT + t) % 2
                emit_ln_tile(kb, xg[:, t, :], R("xg"), gam, bet, scr, so, yo[so][:], R("yo%d" % so), "ln")
                row0 = (g * GT + t) * 128
                outs.append(P.op("sp", lambda e, so=so, row0=row0: e.dma_start(out=y[row0:row0 + 128, :], in_=yo[so][:]),
                                 reads=[R("yo%d" % so)], writes=[R("y")], dma_key="yo%d" % so))
        P.emit(final_waits=[("sp", o) for o in outs[-2:]])
    return nc


def build_gmlp():
    kb = KB()
    nc, P, R = kb.nc, kb.P, kb.R
    x = kb.din("x", [TOK, D])
    w_in = kb.din("w_in", [D, 2 * D])
    sg = kb.din("sg", [1, D])
    sbb = kb.din("sb", [1, D])
    w_s = kb.din("w_s", [8, 128, 128])
    b_s = kb.din("b_s", [1, 1024])
    w_out = kb.din("w_out", [D, D])
    lng = kb.din("lng", [1, D])
    lnb = kb.din("lnb", [1, D])
    identd = kb.din("ident", [128, 128])
    triud = kb.din("triu", [128, 128])
    y = kb.dout("y", [TOK, D])
    GT = 4
    NG = TOK // (128 * GT)
    with kb.st:
        ident = kb.sb("ident", [128, 128], F32)
        triu = kb.sb("triu", [128, 128], F32)
        gam = kb.sb("gam", [128, D], F32)
        bet = kb.sb("bet", [128, D], F32)
        sgam = kb.sb("sgam", [128, D], F32)
        sbet = kb.sb("sbet", [128, D], F32)
        bsb = kb.sb("bsb", [128, 8, 128], F32)
        win_sb = kb.sb("win_sb", [128, 8, 2 * D], BF16)
        wout_sb = kb.sb("wout_sb", [128, 8, D], BF16)
        wsn = kb.sb("wsn", [128, 8, 128], F32)
        wsT = kb.sb("wsT", [128, 8, 128], BF16)
        xg = kb.sb("xg", [128, GT, D], F32)
        xT = kb.sb("xT", [128, 8, GT * 128], BF16)
        uT = kb.sb("uT", [128, 8, GT * 128], F32)
        yT = kb.sb("yT", [128, 8, GT * 128], BF16)
        v = [kb.sb("v%d" % i, [128, D], F32) for i in range(2)]
        vln = [kb.sb("vln%d" % i, [128, D], BF16) for i in range(2)]
        tmp = [kb.sb("tmp%d" % i, [128, 4, 128], F32) for i in range(2)]
        yo = [kb.sb("yo%d" % i, [128, D], F32) for i in range(2)]
        scr = alloc_ln_scratch(kb, "ln")
        scr2 = alloc_ln_scratch(kb, "sln")
        pT = [kb.ps("pT%d" % i, [128, 4, 128]) for i in range(2)]
        pG = [kb.ps("pG%d" % i, [128, 512]) for i in range(2)]
        pM = [kb.ps("pM%d" % i, [128, 4, 128]) for i in range(2)]
        pD = [kb.ps("pD%d" % i, [128, 512]) for i in range(2)]

        P.op("sp", lambda e: e.dma_start(out=ident[:], in_=identd[:, :]), writes=[R("ident")], dma_key="ident")
        P.op("sp", lambda e: e.dma_start(out=triu[:], in_=triud[:, :]), writes=[R("triu")], dma_key="triu")
        P.op("sp", lambda e: e.dma_start(out=wsn[:], in_=w_s.rearrange("g t s -> t g s")), writes=[R("wsn")], dma_key="wsn")
        P.op("sp", lambda e: e.dma_start(out=gam[:], in_=lng.partition_broadcast(128)), writes=[R("gam")], dma_key="gam")
        P.op("sp", lambda e: e.dma_start(out=bet[:], in_=lnb.partition_broadcast(128)), writes=[R("bet")], dma_key="bet")
        P.op("sp", lambda e: e.dma_start(out=sgam[:], in_=sg.partition_broadcast(128)), writes=[R("sgam")], dma_key="sgam")
        P.op("sp", lambda e: e.dma_start(out=sbet[:], in_=sbb.partition_broadcast(128)), writes=[R("sbet")], dma_key="sbet")
        P.op("sp", lambda e: e.dma_start(out=bsb[:].rearrange("p g t -> p (g t)"), in_=b_s.partition_broadcast(128)), writes=[R("bsb")], dma_key="bsb")
        winv = w_in.rearrange("(c p) n -> p c n", p=128)
        for i in range(4):
            P.op("pool", lambda e, i=i: e.dma_start(out=win_sb[:, :, i * 512:(i + 1) * 512], in_=winv[:, :, i * 512:(i + 1) * 512]),
                 writes=[R("win_sb")], dma_key="win%d" % i)
        woutv = w_out.rearrange("(c p) n -> p c n", p=128)
        for i in range(2):
            P.op("pool", lambda e, i=i: e.dma_start(out=wout_sb[:, :, i * 512:(i + 1) * 512], in_=woutv[:, :, i * 512:(i + 1) * 512]),
                 writes=[R("wout_sb")], dma_key="wout%d" % i)
        for g in range(8):
            s = g % 2
            P.op("pe", lambda e, g=g, s=s: e.transpose(pT[s][:, 0, :], wsn[:, g, :], ident[:]), reads=[R("wsn"), R("ident")], writes=[R("pT%d" % s)])
            P.op("dve", lambda e, g=g, s=s: e.tensor_tensor(wsT[:, g, :], pT[s][:, 0, :], triu[:], ALU.mult), reads=[R("pT%d" % s), R("triu")], writes=[R("wsT")])
        outs = []
        gcnt = 0
        vcnt = 0
        mcnt = 0
        dcnt = 0
        for g in range(NG):
            xv = x[g * GT * 128:(g + 1) * GT * 128, :].rearrange("(t p) d -> p t d", p=128)
            P.op("sp", lambda e, xv=xv: e.dma_start(out=xg[:], in_=xv), writes=[R("xg")], dma_key="xg")
            emit_load_transpose(kb, xg, R("xg"), xT, R("xT"), ident, pT, GT)
            for fc in range(8):
                s2 = gcnt % 2
                gcnt += 1
                for kc in range(8):
                    P.op("pe", lambda e, s2=s2, kc=kc, fc=fc: e.matmul(pG[s2][:], win_sb[:, kc, fc * 128:(fc + 1) * 128], xT[:, kc, :],
                                                                        start=(kc == 0), stop=(kc == 7)),
                         reads=[R("win_sb"), R("xT")], writes=[R("pG%d" % s2)])
                P.op("act", lambda e, s2=s2, fc=fc: e.activation(uT[:, fc, :], pG[s2][:], AF.Gelu), reads=[R("pG%d" % s2)], writes=[R("uT")])
            for t in range(GT):
                sv = vcnt % 2
                vcnt += 1
                for half in range(2):
                    s2 = gcnt % 2
                    gcnt += 1
                    for kc in range(8):
                        P.op("pe", lambda e, s2=s2, kc=kc, t=t, half=half: e.matmul(pG[s2][:], xT[:, kc, t * 128:(t + 1) * 128],
                                                                                      win_sb[:, kc, D + half * 512:D + (half + 1) * 512],
                                                                                      start=(kc == 0), stop=(kc == 7)),
                             reads=[R("win_sb"), R("xT")], writes=[R("pG%d" % s2)])
                    P.op("act", lambda e, s2=s2, sv=sv, half=half: e.activation(v[sv][:, half * 512:(half + 1) * 512], pG[s2][:], AF.Gelu),
                         reads=[R("pG%d" % s2)], writes=[R("v%d" % sv)])
                emit_ln_tile(kb, v[sv][:], R("v%d" % sv), sgam, sbet, scr2, sv, vln[sv][:], R("vln%d" % sv), "sln", "sgam", "sbet")
                for gh in range(2):
                    sm = mcnt % 2
                    mcnt += 1
                    for q in range(4):
                        grp = gh * 4 + q
                        P.op("pe", lambda e, sm=sm, q=q, grp=grp, sv=sv: e.matmul(pM[sm][:, q, :], vln[sv][:, grp * 128:(grp + 1) * 128], wsT[:, grp, :],
                                                                                  start=True, stop=True),
                             reads=[R("vln%d" % sv), R("wsT")], writes=[R("pM%d" % sm)])
                    P.op("dve", lambda e, sm=sm, gh=gh: e.tensor_tensor(tmp[sm][:], pM[sm][:], bsb[:, gh * 4:(gh + 1) * 4, :], ALU.add),
                         reads=[R("pM%d" % sm), R("bsb")], writes=[R("tmp%d" % sm)])
                    P.op("pool", lambda e, sm=sm, gh=gh, t=t: e.tensor_tensor(yT[:, gh * 4:(gh + 1) * 4, t * 128:(t + 1) * 128], tmp[sm][:],
                                                                              uT[:, gh * 4:(gh + 1) * 4, t * 128:(t + 1) * 128], ALU.mult),
                         reads=[R("tmp%d" % sm), R("uT")], writes=[R("yT")])
            for t in range(GT):
                for half in range(2):
                    sd = dcnt % 2
                    dcnt += 1
                    for fc in range(8):
                        P.op("pe", lambda e, sd=sd, fc=fc, t=t, half=half: e.matmul(pD[sd][:], yT[:, fc, t * 128:(t + 1) * 128],
                                                                                      wout_sb[:, fc, half * 512:(half + 1) * 512],
                                                                                      start=(fc == 0), stop=(fc == 7)),
                             reads=[R("yT"), R("wout_sb")], writes=[R("pD%d" % sd)])
                    P.op("dve", lambda e, sd=sd, t=t, half=half: e.scalar_tensor_tensor(xg[:, t, half * 512:(half + 1) * 512],
                                                                                          xg[:, t, half * 512:(half + 1) * 512], ALPHA, pD[sd][:],
                                                                                          ALU.mult, ALU.add),
                         reads=[R("pD%d" % sd), R("xg")], writes=[R("xg")])
                so = (g * GT + t) % 2
                emit_ln_tile(kb, xg[:, t, :], R("xg"), gam, bet, scr, so, yo[so][:], R("yo%d" % so), "ln")
                row0 = (g * GT + t) * 128
                outs.append(P.op("sp", lambda e, so=so, row0=row0: e.dma_start(out=y[row0:row0 + 128, :], in_=yo[so][:]),
                                 reads=[R("yo%d" % so)], writes=[R("y")], dma_key="yo%d" % so))
        P.emit(final_waits=[("sp", o) for o in outs[-2:]])
    return nc


NH = 16
DH = 64
NEG = -30000.0


def build_kv():
    kb = KB()
    nc, P, R = kb.nc, kb.P, kb.R
    x = kb.din("x", [TOK, D])
    wk = kb.din("wk", [D, D])
    wv = kb.din("wv", [D, D])
    identd = kb.din("ident", [128, 128])
    KT = kb.dout("KT", [D, TOK], BF16)
    Vo = kb.dout("Vo", [TOK, D], BF16)
    km = kb.dout("km", [D, 8])
    GT = 4
    NG = TOK // (128 * GT)
    with kb.st:
        ident = kb.sb("ident", [128, 128], F32)
        wk_sb = kb.sb("wk_sb", [128, 8, D], BF16)
        wv_sb = kb.sb("wv_sb", [128, 8, D], BF16)
        xg = kb.sb("xg", [128, GT, D], F32)
        xT = kb.sb("xT", [128, 8, GT * 128], BF16)
        ktb = [kb.sb("ktb%d" % i, [128, 512], BF16) for i in range(2)]
        vtb = [kb.sb("vtb%d" % i, [128, D], BF16) for i in range(2)]
        kms = kb.sb("kms", [128, 8, 8], F32)
        kms2 = kb.sb("kms2", [128, 8, 8], F32)
        pT = [kb.ps("pT%d" % i, [128, 4, 128]) for i in range(2)]
        pG = [kb.ps("pG%d" % i, [128, 512]) for i in range(2)]
        P.op("sp", lambda e: e.dma_start(out=ident[:], in_=identd[:, :]), writes=[R("ident")], dma_key="ident")
        wkv = wk.rearrange("(c p) n -> p c n", p=128)
        wvv = wv.rearrange("(c p) n -> p c n", p=128)
        for i in range(2):
            P.op("pool", lambda e, i=i: e.dma_start(out=wk_sb[:, :, i * 512:(i + 1) * 512], in_=wkv[:, :, i * 512:(i + 1) * 512]),
                 writes=[R("wk_sb")], dma_key="wk%d" % i)
            P.op("pool", lambda e, i=i: e.dma_start(out=wv_sb[:, :, i * 512:(i + 1) * 512], in_=wvv[:, :, i * 512:(i + 1) * 512]),
                 writes=[R("wv_sb")], dma_key="wv%d" % i)
        outs = []
        gc = 0
        kc_ = 0
        vc = 0
        for g in range(NG):
            xv = x[g * GT * 128:(g + 1) * GT * 128, :].rearrange("(t p) d -> p t d", p=128)
            P.op("sp", lambda e, xv=xv: e.dma_start(out=xg[:], in_=xv), writes=[R("xg")], dma_key="xg")
            emit_load_transpose(kb, xg, R("xg"), xT, R("xT"), ident, pT, GT)
            import os as _os
            DBG = _os.environ.get("KVDBG", "")
            for fc in range(8 if DBG != "vonly" else 0):
                s2 = gc % 2
                gc += 1
                sk = kc_ % 2
                kc_ += 1
                for kc in range(8):
                    P.op("pe", lambda e, s2=s2, kc=kc, fc=fc: e.matmul(pG[s2][:], wk_sb[:, kc, fc * 128:(fc + 1) * 128], xT[:, kc, :],
                                                                        start=(kc == 0), stop=(kc == 7)),
                         reads=[R("wk_sb"), R("xT")], writes=[R("pG%d" % s2)])
                P.op("act", lambda e, s2=s2, sk=sk: e.copy(ktb[sk][:], pG[s2][:]), reads=[R("pG%d" % s2)], writes=[R("ktb%d" % sk)])
                import os as _os
                if _os.environ.get("KVDBG", "") == "nored":
                    P.op("dve", lambda e, s2=s2, fc=fc, g=g: e.tensor_copy(kms[:, fc, g * 2:(g + 1) * 2], pG[s2][:, 0:2]), reads=[R("pG%d" % s2)], writes=[R("kms")])
                else:
                    P.op("dve", lambda e, s2=s2, fc=fc, g=g: e.tensor_reduce(kms[:, fc, g * 2:(g + 1) * 2], pG[s2][:].rearrange("p (b k) -> p b k", b=2),
                                                                          mybir.AxisListType.X, ALU.add),
                     reads=[R("pG%d" % s2)], writes=[R("kms")])
                outs.append(P.op("sp", lambda e, sk=sk, fc=fc, g=g: e.dma_start(out=KT[fc * 128:(fc + 1) * 128, g * 512:(g + 1) * 512], in_=ktb[sk][:]),
                                 reads=[R("ktb%d" % sk)], writes=[R("KT")], dma_key="ktb%d" % sk))
            for t in range(GT if DBG != "konly" else 0):
                sv = vc % 2
                vc += 1
                for half in range(2):
                    s2 = gc % 2
                    gc += 1
                    for kc in range(8):
                        P.op("pe", lambda e, s2=s2, kc=kc, t=t, half=half: e.matmul(pG[s2][:], xT[:, kc, t * 128:(t + 1) * 128],
                                                                                      wv_sb[:, kc, half * 512:(half + 1) * 512],
                                                                                      start=(kc == 0), stop=(kc == 7)),
                             reads=[R("wv_sb"), R("xT")], writes=[R("pG%d" % s2)])
                    if half == 0:
                        P.op("act", lambda e, s2=s2, sv=sv: e.copy(vtb[sv][:, 0:512], pG[s2][:]), reads=[R("pG%d" % s2)], writes=[R("vtb%d" % sv)])
                    else:
                        P.op("dve", lambda e, s2=s2, sv=sv: e.tensor_copy(vtb[sv][:, 512:1024], pG[s2][:]), reads=[R("pG%d" % s2)], writes=[R("vtb%d" % sv)])
                row0 = (g * GT + t) * 128
                outs.append(P.op("sp", lambda e, sv=sv, row0=row0: e.dma_start(out=Vo[row0:row0 + 128, :], in_=vtb[sv][:]),
                                 reads=[R("vtb%d" % sv)], writes=[R("Vo")], dma_key="vtb%d" % sv))
        if DBG not in ("vonly", "konly"):
            P.op("dve", lambda e: e.tensor_scalar(kms2[:], kms[:], 1.0 / 256.0, None, ALU.mult), reads=[R("kms")], writes=[R("kms2")])
            outs.append(P.op("sp", lambda e: e.dma_start(out=km.rearrange("(c p) b -> p c b", p=128), in_=kms2[:]),
                             reads=[R("kms2")], writes=[R("km")], dma_key="kmo"))
        P.emit(final_waits=[("sp", o) for o in outs[-14:]])
    return nc


def build_moba():
    kb = KB()
    nc, P, R = kb.nc, kb.P, kb.R
    x = kb.din("x", [TOK, D])
    wq = kb.din("wq", [D, D])
    wo = kb.din("wo", [D, D])
    lng = kb.din("lng", [1, D])
    lnb = kb.din("lnb", [1, D])
    identd = kb.din("ident", [128, 128])
    KTv = kb.din("KTv", [NH, DH, 16384], BF16)
    Vv = kb.din("Vv", [NH, 128, 128 * 65], BF16)
    kmv = kb.din("kmv", [DH, NH * 64])
    indd = kb.din("ind", [64, 16384], BF16)
    vbd = kb.din("vb", [1, 1024])
    reld = kb.din("rel", [NH, 32])
    ohd = kb.din("onehot", [33, 768])
    y = kb.dout("y", [TOK, D])
    Rscr = nc.dram_tensor("Rscr", [NH * 512 * 768], F32, kind="Internal")
    Rap = Rscr.ap()
    with kb.st:
        ident = kb.sb("ident", [128, 128], F32)
        identb = kb.sb("identb", [128, 128], BF16)
        ones = kb.sb("ones", [128, 64], F32)
        gam = kb.sb("gam", [128, D], F32)
        bet = kb.sb("bet", [128, D], F32)
        km_sb = kb.sb("km_sb", [128, NH, 64], F32)
        KA = kb.sb("KA", [128, 16384], BF16)
        VA = kb.sb("VA", [128, 128, 65], BF16)
        vb16 = kb.sb("vb16", [128, 16, 64], F32)
        mt = kb.sb("mt", [128, 16, 128], BF16)
        relT = kb.sb("relT", [33, NH], F32)
        relTp = kb.sb("relTp", [33, NH], F32)
        c31b = kb.sb("c31b", [128, NH], F32)
        c31m = kb.sb("c31m", [128, NH], F32)
        relrep = [kb.sb("relrep%d" % i, [33, 128], F32) for i in range(2)]
        oh = kb.sb("oh", [33, 768], F32)
        vecb = [kb.sb("vecb0", [128, 768], F32)] * 2
        xr = [kb.sb("xr%d" % i, [128, D], F32) for i in range(2)]
        xT = kb.sb("xT", [128, 8, TOK], BF16)
        wqs = [kb.sb("wqs%d" % i, [128, 8, DH], BF16) for i in range(2)]
        QA = [kb.sb("QA%d" % i, [128, TOK], BF16) for i in range(2)]
        Qf = kb.sb("Qf", [128, TOK // 2], F32)
        gm = kb.sb("gm", [128, 16, 64], F32)
        top8 = kb.sb("top8", [128, 16, 8], F32)
        thr = kb.sb("thr", [128, 16], F32)
        Tt = kb.sb("Tt", [128, 4, 256], F32)
        Pt = [kb.sb("Pt%d" % i, [128, 2, 256], BF16) for i in range(2)]
        sb32 = [kb.sb("sb32_0", [128, 2, 256], F32)]
        rec = kb.sb("rec", [65, 256], F32)
        bcs = kb.sb("bcs", [64, 256], F32)
        otmp = kb.sb("otmp", [64, 256], BF16)
        oT = kb.sb("oT", [128, 8, TOK], BF16)
        yo = kb.sb("yo", [128, D], F32)
        scr = alloc_ln_scratch(kb, "ln", nslot=1)
        bank = [kb.ps("bank%d" % i, [128, 512]) for i in range(8)]
        bk = lambda i: R("bank%d" % i)

        def v3(i, a):
            return bank[i][:].rearrange("p (a b) -> p a b", a=a)

        dma = lambda eng, out, in_, reads, writes, key, **kw: P.op(eng, lambda e: e.dma_start(out=out, in_=in_, **kw), reads=reads, writes=writes, dma_key=key)
        dma("sp", ident[:], identd[:, :], [], [R("ident")], "ident")
        dma("sp", gam[:], lng.partition_broadcast(128), [], [R("gam")], "gam")
        dma("sp", bet[:], lnb.partition_broadcast(128), [], [R("bet")], "bet")
        dma("sp", km_sb[0:64].rearrange("p h m -> p (h m)"), kmv[:, :], [], [R("km_sb")], "km")
        dma("sp", km_sb[64:128].rearrange("p h m -> p (h m)"), kmv[:, :], [], [R("km_sb2")], "km2")
        for i in range(4):
            dma("sp", KA[64:128, i * 4096:(i + 1) * 4096], indd[:, i * 4096:(i + 1) * 4096], [], [R("KAind")], "ind%d" % i)
        dma("sp", vb16[:].rearrange("p t m -> p (t m)"), vbd.partition_broadcast(128), [], [R("vb16")], "vb16")
        dma("sp", oh[:], ohd[:, :], [], [R("oh")], "oh")
        dma("sp", relT[0:32, :], reld.rearrange("h b -> b h"), [], [R("relT")], "relT", allow_slow_non_contiguous=True)
        dma("sp", c31b[:], reld[:, 31:32].rearrange("h o -> o h").partition_broadcast(128), [], [R("c31b")], "c31b", allow_slow_non_contiguous=True)
        P.op("dve", lambda e: e.tensor_copy(identb[:], ident[:]), reads=[R("ident")], writes=[R("identb")])
        P.op("dve", lambda e: e.memset(ones[:], 1.0), writes=[R("ones")])
        P.op("dve", lambda e: e.memset(mt[:], 0.0), writes=[R("mt")])
        P.op("dve", lambda e: e.memset(relTp[32:33, :], NEG), writes=[R("relTp32")])
        P.op("dve", lambda e: e.tensor_scalar(c31m[:], c31b[:], NEG, None, ALU.add), reads=[R("c31b")], writes=[R("c31m")])
        P.op("dve", lambda e: e.tensor_tensor(relTp[0:32, :], relT[0:32, :], c31b[0:32, :], ALU.subtract), reads=[R("relT"), R("c31b")], writes=[R("relTp")])
        for h in range(NH):
            s = 0
            rs_ = h % 2
            b0, b1 = 4 + rs_, 6 + rs_
            P.op("dve", lambda e, h=h, rs_=rs_: e.tensor_copy(relrep[rs_][:], relTp[:, h:h + 1].to_broadcast([33, 128])),
                 reads=[R("relTp"), R("relTp32")], writes=[R("relrep%d" % rs_)])
            P.op("pe", lambda e, rs_=rs_, b0=b0: e.matmul(bank[b0][:], relrep[rs_][:], oh[:, 0:512], start=True, stop=True),
                 reads=[R("relrep%d" % rs_), R("oh")], writes=[bk(b0)])
            P.op("pe", lambda e, rs_=rs_, b1=b1: e.matmul(bank[b1][:, 0:256], relrep[rs_][:], oh[:, 512:768], start=True, stop=True),
                 reads=[R("relrep%d" % rs_), R("oh")], writes=[bk(b1)])
            P.op("act", lambda e, s=s, b0=b0: e.copy(vecb[s][:, 0:512], bank[b0][:]), reads=[bk(b0)], writes=[R("vecb%d" % s)])
            P.op("dve", lambda e, s=s, b1=b1: e.tensor_copy(vecb[s][:, 512:768], bank[b1][:, 0:256]), reads=[bk(b1)], writes=[R("vecb%d" % s)])
            for r4 in range(4):
                off = h * 512 * 768 + r4 * 128 * 768
                dst = bass.AP(Rscr, off, [[768, 128], [1, 768]])
                dma("sp", dst, vecb[s][:], [R("vecb%d" % s)], [R("Rscr%d" % h)], "rs%d_%d" % (s, r4))
        ecnt = [0]
        for t in range(16):
            s = t % 2
            dma("sp", xr[s][:], x[t * 128:(t + 1) * 128, :], [], [R("xr%d" % s)], "xr%d" % s)
            for h4 in range(2):
                bi = 4 + (ecnt[0] % 2)
                ecnt[0] += 1
                for q in range(4):
                    kc = h4 * 4 + q
                    P.op("pe", lambda e, s=s, kc=kc, q=q, bi=bi: e.transpose(v3(bi, 4)[:, q, :], xr[s][:, kc * 128:(kc + 1) * 128], ident[:]),
                         reads=[R("xr%d" % s), R("ident")], writes=[bk(bi)])
                if h4 == 0:
                    P.op("dve", lambda e, t=t, h4=h4, bi=bi: e.tensor_copy(xT[:, h4 * 4:(h4 + 1) * 4, t * 128:(t + 1) * 128], v3(bi, 4)),
                         reads=[bk(bi)], writes=[R("xT")])
                else:
                    P.op("act", lambda e, t=t, h4=h4, bi=bi: e.copy(xT[:, h4 * 4:(h4 + 1) * 4, t * 128:(t + 1) * 128], v3(bi, 4)),
                         reads=[bk(bi)], writes=[R("xT")])
        wqv = wq.rearrange("(c p) n -> p c n", p=128)
        C = dict(m=0, s=0, p=0, o=0)

        def misc_bank():
            bi = 4 + (C["m"] % 2)
            C["m"] += 1
            return bi

        def pre(h):
            hs = h % 2
            rQA = R("QA%d" % hs)
            dma("pool", wqs[hs][:], wqv[:, :, h * DH:(h + 1) * DH], [], [R("wqs%d" % hs)], "wqs%d" % hs)
            for tg in range(4):
                bi = misc_bank()
                for kc in range(8):
                    P.op("pe", lambda e, bi=bi, kc=kc, tg=tg, hs=hs: e.matmul(bank[bi][0:64, :], wqs[hs][:, kc, :], xT[:, kc, tg * 512:(tg + 1) * 512],
                                                                              start=(kc == 0), stop=(kc == 7)),
                         reads=[R("wqs%d" % hs), R("xT")], writes=[bk(bi)])
                P.op("act", lambda e, bi=bi, tg=tg, hs=hs: e.mul(QA[hs][0:64, tg * 512:(tg + 1) * 512], bank[bi][0:64, :], 0.125),
                     reads=[bk(bi)], writes=[rQA])
                P.op("dve", lambda e, bi=bi, tg=tg: e.tensor_scalar(Qf[(tg // 2) * 64:(tg // 2) * 64 + 64, (tg % 2) * 512:(tg % 2 + 1) * 512], bank[bi][0:64, :], 0.125, None, ALU.mult),
                     reads=[bk(bi)], writes=[R("Qf")])
            for t in range(16):
                gb = 6 + t // 8
                pb = (t // 8) * 64
                P.op("pe", lambda e, t=t, gb=gb, h=h, pb=pb: e.matmul(bank[gb][:, (t % 8) * 64:(t % 8 + 1) * 64], Qf[pb:pb + 64, (t % 8) * 128:(t % 8 + 1) * 128],
                                                                       km_sb[pb:pb + 64, h, :], start=True, stop=True),
                     reads=[R("Qf"), R("km_sb"), R("km_sb2")], writes=[bk(gb)])
            for g2 in range(2):
                P.op("dve", lambda e, g2=g2: e.tensor_tensor(gm[:, g2 * 8:(g2 + 1) * 8, :], v3(6 + g2, 8), vb16[:, g2 * 8:(g2 + 1) * 8, :], ALU.add),
                     reads=[bk(6 + g2), R("vb16")], writes=[R("gm")])
            for t in range(16):
                P.op("dve", lambda e, t=t: e.max(out=top8[:, t, :], in_=gm[:, t, :]), reads=[R("gm")], writes=[R("top8")])
            P.op("dve", lambda e: e.tensor_scalar(thr[:], top8[:, :, 3], -1e29, None, ALU.max), reads=[R("top8")], writes=[R("thr")])
            P.op("dve", lambda e: e.tensor_tensor(gm[:], gm[:], thr[:].unsqueeze(2).to_broadcast([128, 16, 64]), ALU.is_ge),
                 reads=[R("gm"), R("thr")], writes=[R("gm")])
            P.op("dve", lambda e, h=h: e.tensor_scalar(mt[:, :, 64:128], gm[:], -NEG, c31m[:, h:h + 1], ALU.mult, ALU.add),
                 reads=[R("gm"), R("c31m")], writes=[R("mt")])
            for t4 in range(4):
                bi = misc_bank()
                for q in range(4):
                    t = t4 * 4 + q
                    P.op("pe", lambda e, bi=bi, q=q, t=t: e.matmul(v3(bi, 4)[:, q, :], mt[:, t, :], identb[:], start=True, stop=True),
                         reads=[R("mt"), R("identb")], writes=[bk(bi)])
                if t4 % 2 == 0:
                    P.op("act", lambda e, bi=bi, t4=t4, hs=hs: e.copy(QA[hs][64:128, t4 * 512:(t4 + 1) * 512], bank[bi][64:128, :]),
                         reads=[bk(bi)], writes=[rQA])
                else:
                    P.op("dve", lambda e, bi=bi, t4=t4, hs=hs: e.tensor_copy(QA[hs][64:128, t4 * 512:(t4 + 1) * 512], bank[bi][64:128, :]),
                         reads=[bk(bi)], writes=[rQA])

        def loads(h):
            for cc in range(4):
                src = bass.AP(Rscr, h * 512 * 768 + 511 + cc * 128 * 767, [[767, 128], [1, 256]])
                dma("sp", Tt[:, cc, :], src, [R("Rscr%d" % h)], [R("Tt%d" % (cc // 2))], "Tt%d" % cc)
            VAf = VA[:].rearrange("p c d -> p (c d)")
            for i in range(4):
                dma("sp", KA[0:64, i * 4096:(i + 1) * 4096], KTv[h, :, i * 4096:(i + 1) * 4096], [], [R("KA%d" % i)], "KA%d" % i)
                dma("sp", VAf[:, i * 2080:(i + 1) * 2080], Vv[h, :, i * 2080:(i + 1) * 2080], [], [R("VA%d" % i)], "VA%d" % i)

        def attn(h, mid=None):
            hs = h % 2
            rQA = R("QA%d" % hs)
            for j in range(8):
                if j == 4 and mid is not None:
                    mid()
                mown = 8 * j + 7
                ob = 2 + (C["o"] % 2)
                C["o"] += 1
                nmm = 2 * (mown + 1)
                imm = 0
                for m in range(mown + 1):
                    sbk = C["s"] % 2
                    C["s"] += 1
                    for half in range(2):
                        kc = 2 * m + half
                        P.op("pe", lambda e, sbk=sbk, half=half, kc=kc, j=j, hs=hs: e.matmul(v3(sbk, 2)[:, half, :], KA[:, kc * 128:(kc + 1) * 128],
                                                                                           QA[hs][:, j * 256:(j + 1) * 256], start=True, stop=True),
                             reads=[R("KA%d" % (kc // 32)), R("KAind"), rQA], writes=[bk(sbk)])
                    sp_ = C["p"] % 2
                    C["p"] += 1
                    if m >= mown - 1:
                        c0 = (m - (mown - 1)) * 2
                        P.op("dve", lambda e, sbk=sbk, c0=c0: e.tensor_tensor(sb32[0][:], v3(sbk, 2), Tt[:, c0:c0 + 2, :], ALU.add),
                             reads=[bk(sbk), R("Tt%d" % (c0 // 2))], writes=[R("sb32_0")])
                        P.op("act", lambda e, sp_=sp_: e.activation(Pt[sp_][:], sb32[0][:], AF.Exp),
                             reads=[R("sb32_0")], writes=[R("Pt%d" % sp_)])
                    else:
                        P.op("act", lambda e, sp_=sp_, sbk=sbk: e.activation(Pt[sp_][:], v3(sbk, 2), AF.Exp),
                             reads=[bk(sbk)], writes=[R("Pt%d" % sp_)])
                    for half in range(2):
                        kc = 2 * m + half
                        P.op("pe", lambda e, ob=ob, kc=kc, sp_=sp_, half=half, imm=imm, nmm=nmm: e.matmul(bank[ob][0:65, 0:256], VA[:, kc, :], Pt[sp_][:, half, :],
                                                                                                        start=(imm == 0), stop=(imm == nmm - 1)),
                             reads=[R("VA%d" % (kc // 32)), R("Pt%d" % sp_)], writes=[bk(ob)])
                        imm += 1
                P.op("dve", lambda e, ob=ob: e.reciprocal(rec[64:65, :], bank[ob][64:65, 0:256]), reads=[bk(ob)], writes=[R("rec")])
                bb = misc_bank()
                P.op("pe", lambda e, bb=bb: e.matmul(bank[bb][0:64, 0:256], ones[64:65, 0:64], rec[64:65, :], start=True, stop=True),
                     reads=[R("ones"), R("rec")], writes=[bk(bb)])
                P.op("act", lambda e, bb=bb: e.copy(bcs[:], bank[bb][0:64, 0:256]), reads=[bk(bb)], writes=[R("bcs")])
                if hs == 0:
                    P.op("dve", lambda e, ob=ob, h=h, j=j: e.tensor_tensor(oT[0:64, h // 2, j * 256:(j + 1) * 256], bank[ob][0:64, 0:256], bcs[:], ALU.mult),
                         reads=[bk(ob), R("bcs")], writes=[R("oT")])
                else:
                    P.op("dve", lambda e, ob=ob: e.tensor_tensor(otmp[:], bank[ob][0:64, 0:256], bcs[:], ALU.mult),
                         reads=[bk(ob), R("bcs")], writes=[R("otmp")])
                    P.op("act", lambda e, h=h, j=j: e.copy(oT[64:128, h // 2, j * 256:(j + 1) * 256], otmp[:]),
                         reads=[R("otmp")], writes=[R("oT")])

        NHRUN = NH
        pre(0)
        loads(0)
        for h in range(NHRUN):
            attn(h, (lambda h=h: pre(h + 1)) if h + 1 < NHRUN else None)
            if h + 1 < NHRUN:
                loads(h + 1)
        outs = []
        wo_sb = KA[:, 0:8192].rearrange("p (c n) -> p c n", c=8)
        wov = wo.rearrange("(c p) n -> p c n", p=128)
        for i in range(2):
            dma("pool", wo_sb[:, :, i * 512:(i + 1) * 512], wov[:, :, i * 512:(i + 1) * 512], [], [R("KA0"), R("KA1"), R("KAind")], "wo%d" % i)
        dcnt = 0
        for t in range(16):
            s = t % 2
            dma("sp", xr[s][:], x[t * 128:(t + 1) * 128, :], [], [R("xr%d" % s)], "xr%d" % s)
            for half in range(2):
                sd = dcnt % 2
                dcnt += 1
                for fc in range(8):
                    P.op("pe", lambda e, sd=sd, fc=fc, t=t, half=half: e.matmul(bank[sd][:], oT[:, fc, t * 128:(t + 1) * 128],
                                                                                  wo_sb[:, fc, half * 512:(half + 1) * 512],
                                                                                  start=(fc == 0), stop=(fc == 7)),
                         reads=[R("oT"), R("KA0"), R("KA1"), R("KAind")], writes=[bk(sd)])
                P.op("dve", lambda e, sd=sd, s=s, half=half: e.scalar_tensor_tensor(xr[s][:, half * 512:(half + 1) * 512],
                                                                                      xr[s][:, half * 512:(half + 1) * 512], ALPHA, bank[sd][:],
                                                                                      ALU.mult, ALU.add),
                     reads=[bk(sd), R("xr%d" % s)], writes=[R("xr%d" % s)])
            emit_ln_tile(kb, xr[s][:], R("xr%d" % s), gam, bet, scr, 0, yo[:], R("yo"), "ln")
            outs.append(dma("sp", y[t * 128:(t + 1) * 128, :], yo[:], [R("yo")], [R("y")], "yo"))
        P.emit(final_waits=[("sp", outs[-1])])
    return nc


import numpy as np
import ml_dtypes
BF = ml_dtypes.bfloat16
NEG_BIG = -1e30


def shard_tokens(a):
    b = a.reshape(8, 8, 256, *a.shape[1:])
    return [np.ascontiguousarray(b[:, c].reshape(2048, *a.shape[1:])) for c in range(8)]


def unshard_tokens(lst):
    b = np.stack([l.reshape(8, 256, *l.shape[1:]) for l in lst], axis=1)
    return b.reshape(16384, *lst[0].shape[1:])


def t5_bucket_np(d):
    import jax, jax.numpy as jnp
    import math
    cpu = jax.devices("cpu")[0]
    with jax.default_device(cpu):
        n = jnp.maximum(jnp.asarray(d, dtype=jnp.int32), 0)
        max_exact = 16
        nf = jnp.maximum(n, 1).astype(jnp.float32)
        large = max_exact + (jnp.log(nf / max_exact) / math.log(128 / max_exact) * (32 - max_exact)).astype(jnp.int32)
        large = jnp.minimum(large, 31)
        return np.asarray(jnp.where(n < max_exact, n, large))


def moba_consts():
    ident = np.eye(128, dtype=np.float32)
    ind = np.zeros((64, 16384), dtype=BF)
    for m in range(64):
        ind[m, m * 256:(m + 1) * 256] = 1
    onehot = np.zeros((33, 768), dtype=np.float32)
    dd = np.arange(767) - 255
    bk = t5_bucket_np(np.maximum(dd, 0))
    for i in range(767):
        if dd[i] < 0:
            onehot[32, i] = 1
        else:
            onehot[bk[i], i] = 1
    return ident, ind, onehot


def valid_bias(c):
    vb = np.full((16, 64), NEG_BIG, dtype=np.float32)
    for t in range(16):
        j = t // 2
        lo, hi = 7 - c, 8 * j + 7
        if hi > lo:
            vb[t, lo:hi] = 0.0
        vb[t, hi] = -NEG_BIG
    return vb.reshape(1, 1024)


def gather_kv(KTs, Vs, kms):
    KT_all = np.zeros((1024, 64, 256), dtype=BF)
    V_all = np.zeros((64, 256, 1024), dtype=BF)
    km_all = np.zeros((1024, 64), dtype=np.float32)
    for c in range(8):
        KT_all[:, c::8, :] = KTs[c].reshape(1024, 8, 256)
        V_all[c::8] = Vs[c].reshape(8, 256, 1024)
        km_all[:, c::8] = kms[c]
    views = []
    for c in range(8):
        KTv = np.zeros((1024, 64, 256), dtype=BF)
        Vb = np.zeros((64, 256, 1024), dtype=BF)
        kmv = np.zeros((1024, 64), dtype=np.float32)
        n = 57 + c
        KTv[:, 7 - c:, :] = KT_all[:, :n, :]
        Vb[7 - c:] = V_all[:n]
        kmv[:, 7 - c:] = km_all[:, :n]
        KTv = KTv.reshape(16, 64, 16384)
        Vv = np.ones((16, 128, 128, 65), dtype=BF)
        Vv[:, :, :, :64] = Vb.reshape(128, 128, 16, 64).transpose(2, 1, 0, 3)
        kmv2 = np.ascontiguousarray(kmv.reshape(16, 64, 64).transpose(1, 0, 2).reshape(64, 1024))
        views.append((np.ascontiguousarray(KTv), np.ascontiguousarray(Vv.reshape(16, 128, 128 * 65)), kmv2))
    return views


_PROGS = {}


def _prog(name):
    if name not in _PROGS:
        _PROGS[name] = {"ffn": build_ffn, "gmlp": build_gmlp, "kv": build_kv, "moba": build_moba}[name]()
    return _PROGS[name]


def _run(name, maps):
    res = run_bass_kernel_spmd(_prog(name), maps, core_ids=list(range(8)))
    return res.results


def kernel(x, ln_g, ln_b, ffn_w_gate, ffn_w_up, ffn_w_down, gm_w_in, gm_sgu_g, gm_sgu_b, gm_w_s, gm_b_s, gm_w_out,
           attn_w_q, attn_w_o, w_k_shared, w_v_shared, rel_bias):
    f = lambda a: np.ascontiguousarray(np.asarray(a, dtype=np.float32))
    x, ln_g, ln_b = f(x), f(ln_g), f(ln_b)
    ffn_w_gate, ffn_w_up, ffn_w_down = f(ffn_w_gate), f(ffn_w_up), f(ffn_w_down)
    gm_w_in, gm_sgu_g, gm_sgu_b, gm_w_s, gm_b_s, gm_w_out = f(gm_w_in), f(gm_sgu_g), f(gm_sgu_b), f(gm_w_s), f(gm_b_s), f(gm_w_out)
    attn_w_q, attn_w_o, w_k_shared, w_v_shared, rel_bias = f(attn_w_q), f(attn_w_o), f(w_k_shared), f(w_v_shared), f(rel_bias)
    ident, ind, onehot = moba_consts()
    triu = np.triu(np.ones((128, 128), np.float32))
    xs = shard_tokens(x[0])
    views = None

    def ffn(xs, l, i):
        r = _run("ffn", [dict(x=xs[c], wg=ffn_w_gate[l, i], wu=ffn_w_up[l, i], wd=ffn_w_down[l, i], lng=ln_g[l, 2 * i][None],
                              lnb=ln_b[l, 2 * i][None], ident=ident) for c in range(8)])
        return [q["y"] for q in r]

    for l in range(4):
        xs = ffn(xs, l, 0)
        if l < 2:
            r = _run("gmlp", [dict(x=xs[c], w_in=gm_w_in[l], sg=gm_sgu_g[l][None], sb=gm_sgu_b[l][None], w_s=gm_w_s[l],
                                   b_s=gm_b_s[l].reshape(1, 1024), w_out=gm_w_out[l], lng=ln_g[l, 1][None], lnb=ln_b[l, 1][None],
                                   ident=ident, triu=triu) for c in range(8)])
        else:
            a = l - 2
            r = _run("moba", [dict(x=xs[c], wq=attn_w_q[a], wo=attn_w_o[a], lng=ln_g[l, 1][None], lnb=ln_b[l, 1][None], ident=ident,
                                   KTv=views[c][0], Vv=views[c][1], kmv=views[c][2], ind=ind, vb=valid_bias(c), rel=rel_bias,
                                   onehot=onehot) for c in range(8)])
        xs = [q["y"] for q in r]
        xs = ffn(xs, l, 1)
        if l == 1:
            r = _run("kv", [dict(x=xs[c], wk=w_k_shared, wv=w_v_shared, ident=ident) for c in range(8)])
            views = gather_kv([q["KT"] for q in r], [q["Vo"] for q in r], [q["km"] for q in r])
    return unshard_tokens(xs)[None].astype(np.float32)
```
